# Optimizing a Trainium2 kernel written in Bass

```python
import jax
import jax.numpy as jnp
from jax import lax
import numpy as np

D_MODEL = 2048
BATCH = 2
SEQ = 8192
DEPTH = 4

GRID_W = 64
CTX_LEN = 256
N_MOD = 9
D_FF = 5632
D_A = 1024
CONV_A = 31
HB = 8
Q_LORA = 512
KV_LORA = 512
NOPE_B = 128
ROPE_B = 64
V_B = 128
QK_B = NOPE_B + ROPE_B
HC = 8
HKV_C = 2
DH_C = 128
GROUP_C = HC // HKV_C
D_D = 1024
CONV_D = 3
N_BRANCH = 4
Q_BLOCK = 128
ROPE_BASE = 10000.0
EPS = 1e-6
SPLIT_SIZES = (KV_LORA, ROPE_B, HKV_C * DH_C, HKV_C * DH_C, Q_LORA, HC * DH_C,
               2 * D_A, 3 * D_D, N_BRANCH * D_MODEL)
N_CTX_KV_PIECES = 4
KV_COLS = KV_LORA + ROPE_B + 2 * HKV_C * DH_C
IN_COLS = KV_COLS + Q_LORA + HC * DH_C + 2 * D_A + 3 * D_D + N_BRANCH * D_MODEL

kernel_name = "hybrid_parallel_mixer_dit"


def _split(p, sizes):
    out, start = [], 0
    for size in sizes:
        out.append(p[..., start:start + size])
        start += size
    return out


def _rms(x, gain=None):
    xf = x.astype(jnp.float32)
    y = (xf * lax.rsqrt(jnp.mean(xf * xf, axis=-1, keepdims=True) + EPS)).astype(x.dtype)
    return y if gain is None else y * gain


def _layer_norm(x, gain, bias):
    xf = x.astype(jnp.float32)
    mu = jnp.mean(xf, axis=-1, keepdims=True)
    var = jnp.mean(jnp.square(xf - mu), axis=-1, keepdims=True)
    return ((xf - mu) * lax.rsqrt(var + EPS)).astype(x.dtype) * gain + bias


def _modulate(x, shift, scale):
    return _rms(x) * (1 + scale) + shift


def _swiglu(x, w_in, w_out):
    a, b = jnp.split(x @ w_in, 2, axis=-1)
    return (jax.nn.silu(a) * b) @ w_out


def _dwconv(x, w):
    return lax.conv_general_dilated(
        x, w[:, None, :], window_strides=(1,), padding="SAME",
        dimension_numbers=("NWC", "WIO", "NWC"), feature_group_count=x.shape[-1])


def _axial_rope_tables(n_tokens, d_rot):
    t = jnp.arange(n_tokens)
    row = (t // GRID_W).astype(jnp.float32)
    col = (t % GRID_W).astype(jnp.float32)
    n_freq = d_rot // 4
    inv = ROPE_BASE ** (-jnp.arange(n_freq, dtype=jnp.float32) / n_freq)
    ang = jnp.concatenate([row[:, None] * inv, col[:, None] * inv], axis=-1)
    return jnp.cos(ang), jnp.sin(ang)


def _apply_rope(x, rope):
    cos, sin = rope
    cos = cos[None, :, None, :].astype(x.dtype)
    sin = sin[None, :, None, :].astype(x.dtype)
    x1, x2 = jnp.split(x, 2, axis=-1)
    return jnp.concatenate([x1 * cos - x2 * sin, x2 * cos + x1 * sin], axis=-1)


def _attend(q, k, v):
    bsz, lq, hkv, grp, dk = q.shape
    nb = lq // Q_BLOCK
    qb = jnp.moveaxis(q.reshape(bsz, nb, Q_BLOCK, hkv, grp, dk), 1, 0)

    def one_block(qblk):
        s = jnp.einsum("bqhgd,bkhd->bhgqk", qblk, k, preferred_element_type=jnp.float32)
        p = jax.nn.softmax(s, axis=-1).astype(v.dtype)
        return jnp.einsum("bhgqk,bkhd->bqhgd", p, v)

    o = lax.map(one_block, qb)
    return jnp.moveaxis(o, 0, 1).reshape(bsz, lq, hkv * grp * v.shape[-1])


def _mla_kv(ckv, krope, lp, rope):
    bsz, n, _ = ckv.shape
    kv = (_rms(ckv, lp["g_kv_lora"]) @ lp["w_ukv"]).reshape(bsz, n, HB, NOPE_B + V_B)
    k_nope, v = kv[..., :NOPE_B], kv[..., NOPE_B:]
    k_pe = jnp.broadcast_to(krope[:, :, None, :], (bsz, n, HB, ROPE_B))
    k = _rms(jnp.concatenate([k_nope, k_pe], axis=-1), lp["g_k_b"])
    if rope is not None:
        k = jnp.concatenate([k[..., :NOPE_B], _apply_rope(k[..., NOPE_B:], rope)], axis=-1)
    return k, v


def _mla_q(cq, lp, rope):
    bsz, n, _ = cq.shape
    q = (_rms(cq, lp["g_q_lora"]) @ lp["w_uq"]).reshape(bsz, n, HB, QK_B)
    q = _rms(q, lp["g_q_b"])
    if rope is not None:
        q = jnp.concatenate([q[..., :NOPE_B], _apply_rope(q[..., NOPE_B:], rope)], axis=-1)
    return (q * (QK_B ** -0.5))[:, :, :, None, :]


def _gqa_kv(k, v, lp, rope):
    bsz, n, _ = k.shape
    k = _rms(k.reshape(bsz, n, HKV_C, DH_C), lp["g_k_c"])
    if rope is not None:
        k = _apply_rope(k, rope)
    return k, v.reshape(bsz, n, HKV_C, DH_C)


def _gqa_q(q, lp, rope):
    bsz, n, _ = q.shape
    q = _rms(q.reshape(bsz, n, HC, DH_C), lp["g_q_c"])
    if rope is not None:
        q = _apply_rope(q, rope)
    return (q * (DH_C ** -0.5)).reshape(bsz, n, HKV_C, GROUP_C, DH_C)


def _conformer_conv(u, lp):
    a = u[..., :D_A] * jax.nn.sigmoid(u[..., D_A:])
    a = _dwconv(a, lp["w_dw_a"]) + lp["b_dw_a"]
    a = jax.nn.silu(_layer_norm(a, lp["g_ln_a"], lp["b_ln_a"]))
    return a @ lp["w_out_a"]


def _short_conv(u, lp):
    bg, cg, h = jnp.split(u, 3, axis=-1)
    return (bg * _dwconv(cg * h, lp["w_dw_d"])) @ lp["w_out_d"]


def _merge(gate_logits, ys, w_o):
    g = jax.nn.sigmoid(gate_logits).reshape(gate_logits.shape[:-1] + (N_BRANCH, D_MODEL))
    y = g[..., 0, :] * ys[0]
    for i in range(1, N_BRANCH):
        y = y + g[..., i, :] * ys[i]
    return y @ w_o


def _token_mixer(xl, xc, lp, rope_b, rope_c, ctx_out):
    (ckv_l, kr_l, k_l, v_l, cq_l, q_l, ua_l, ud_l, gt_l) = _split(xl @ lp["w_in"], SPLIT_SIZES)
    if ctx_out:
        pieces_c = _split(xc @ lp["w_in"], SPLIT_SIZES)
    else:
        pieces_c = _split(xc @ lp["w_in"][:, :KV_COLS], SPLIT_SIZES[:N_CTX_KV_PIECES])
    ckv_c, kr_c, k_c, v_c = pieces_c[:N_CTX_KV_PIECES]

    kb_c, vb_c = _mla_kv(ckv_c, kr_c, lp, None)
    kb_l, vb_l = _mla_kv(ckv_l, kr_l, lp, rope_b)
    kc_c, vc_c = _gqa_kv(k_c, v_c, lp, None)
    kc_l, vc_l = _gqa_kv(k_l, v_l, lp, rope_c)

    yb_l = _attend(_mla_q(cq_l, lp, rope_b),
                   jnp.concatenate([kb_c, kb_l], axis=1),
                   jnp.concatenate([vb_c, vb_l], axis=1)) @ lp["w_o_b"]
    yc_l = _attend(_gqa_q(q_l, lp, rope_c),
                   jnp.concatenate([kc_c, kc_l], axis=1),
                   jnp.concatenate([vc_c, vc_l], axis=1)) @ lp["w_o_c"]
    ya_l = _conformer_conv(ua_l, lp)
    yd_l = _short_conv(ud_l, lp)
    y_l = _merge(gt_l, (ya_l, yb_l, yc_l, yd_l), lp["w_o"])
    if not ctx_out:
        return y_l, None

    cq_c, q_c, ua_c, ud_c, gt_c = pieces_c[N_CTX_KV_PIECES:]
    yb_c = _attend(_mla_q(cq_c, lp, None), kb_c, vb_c) @ lp["w_o_b"]
    yc_c = _attend(_gqa_q(q_c, lp, None), kc_c, vc_c) @ lp["w_o_c"]
    ya_c = _conformer_conv(ua_c, lp)
    yd_c = _short_conv(ud_c, lp)
    y_c = _merge(gt_c, (ya_c, yb_c, yc_c, yd_c), lp["w_o"])
    return y_l, y_c


def setup_inputs(seed: int = 0) -> dict:
    key = jax.random.key(seed)
    keys = jax.random.split(key, 32)
    counter = [0]

    def nrm(shape, scale):
        k = keys[counter[0]]
        counter[0] += 1
        return jax.random.normal(k, shape, jnp.float32) * scale

    L = DEPTH
    D = D_MODEL
    return {
        "x": nrm((BATCH, SEQ, D), 1.0),
        "c": nrm((BATCH, D), 1.0),
        "ctx": nrm((BATCH, CTX_LEN, D), 1.0),
        "c_ctx": nrm((D,), 1.0),
        "w_mod": nrm((L, D, N_MOD * D), 0.5 * D ** -0.5),
        "b_mod": nrm((L, N_MOD * D), 0.01),
        "w_ffn1_in": nrm((L, D, 2 * D_FF), D ** -0.5),
        "w_ffn1_out": nrm((L, D_FF, D), D_FF ** -0.5),
        "w_ffn2_in": nrm((L, D, 2 * D_FF), D ** -0.5),
        "w_ffn2_out": nrm((L, D_FF, D), D_FF ** -0.5),
        "w_in": nrm((L, D, IN_COLS), D ** -0.5),
        "g_q_lora": 1.0 + nrm((L, Q_LORA), 0.02),
        "w_uq": nrm((L, Q_LORA, HB * QK_B), Q_LORA ** -0.5),
        "g_kv_lora": 1.0 + nrm((L, KV_LORA), 0.02),
        "w_ukv": nrm((L, KV_LORA, HB * (NOPE_B + V_B)), KV_LORA ** -0.5),
        "g_q_b": 1.0 + nrm((L, QK_B), 0.02),
        "g_k_b": 1.0 + nrm((L, QK_B), 0.02),
        "w_o_b": nrm((L, HB * V_B, D), (HB * V_B) ** -0.5),
        "g_q_c": 1.0 + nrm((L, DH_C), 0.02),
        "g_k_c": 1.0 + nrm((L, DH_C), 0.02),
        "w_o_c": nrm((L, HC * DH_C, D), (HC * DH_C) ** -0.5),
        "w_dw_a": nrm((L, CONV_A, D_A), CONV_A ** -0.5),
        "b_dw_a": nrm((L, D_A), 0.01),
        "g_ln_a": 1.0 + nrm((L, D_A), 0.02),
        "b_ln_a": nrm((L, D_A), 0.01),
        "w_out_a": nrm((L, D_A, D), D_A ** -0.5),
        "w_dw_d": nrm((L, CONV_D, D_D), CONV_D ** -0.5),
        "w_out_d": nrm((L, D_D, D), D_D ** -0.5),
        "w_o": nrm((L, D, D), D ** -0.5),
    }


def reference(x, c, ctx, c_ctx, w_mod, b_mod, w_ffn1_in, w_ffn1_out, w_ffn2_in, w_ffn2_out,
              w_in, g_q_lora, w_uq, g_kv_lora, w_ukv, g_q_b, g_k_b, w_o_b, g_q_c, g_k_c, w_o_c,
              w_dw_a, b_dw_a, g_ln_a, b_ln_a, w_out_a, w_dw_d, w_out_d, w_o):
    bsz, n_lat, _ = x.shape
    rope_b = _axial_rope_tables(n_lat, ROPE_B)
    rope_c = _axial_rope_tables(n_lat, DH_C)
    silu_c = jax.nn.silu(c)
    silu_cc = jax.nn.silu(c_ctx)
    hl, hc = x, ctx
    for i in range(DEPTH):
        last = i == DEPTH - 1
        lp = {
            "w_in": w_in[i], "g_q_lora": g_q_lora[i], "w_uq": w_uq[i],
            "g_kv_lora": g_kv_lora[i], "w_ukv": w_ukv[i], "g_q_b": g_q_b[i], "g_k_b": g_k_b[i],
            "w_o_b": w_o_b[i], "g_q_c": g_q_c[i], "g_k_c": g_k_c[i], "w_o_c": w_o_c[i],
            "w_dw_a": w_dw_a[i], "b_dw_a": b_dw_a[i], "g_ln_a": g_ln_a[i], "b_ln_a": b_ln_a[i],
            "w_out_a": w_out_a[i], "w_dw_d": w_dw_d[i], "w_out_d": w_out_d[i], "w_o": w_o[i],
        }
        mod_l = (silu_c @ w_mod[i] + b_mod[i]).reshape(bsz, N_MOD, 1, D_MODEL)
        ml = [mod_l[:, j] for j in range(N_MOD)]
        mc = (silu_cc @ w_mod[i] + b_mod[i]).reshape(N_MOD, D_MODEL)

        hl = hl + 0.5 * ml[2] * _swiglu(_modulate(hl, ml[0], ml[1]), w_ffn1_in[i], w_ffn1_out[i])
        hc = hc + 0.5 * mc[2] * _swiglu(_modulate(hc, mc[0], mc[1]), w_ffn1_in[i], w_ffn1_out[i])

        y_l, y_c = _token_mixer(_modulate(hl, ml[3], ml[4]), _modulate(hc, mc[3], mc[4]),
                                lp, rope_b, rope_c, not last)
        hl = hl + ml[5] * y_l

        hl = hl + 0.5 * ml[8] * _swiglu(_modulate(hl, ml[6], ml[7]), w_ffn2_in[i], w_ffn2_out[i])
        if not last:
            hc = hc + mc[5] * y_c
            hc = hc + 0.5 * mc[8] * _swiglu(_modulate(hc, mc[6], mc[7]), w_ffn2_in[i], w_ffn2_out[i])
    return hl
```

```python
import os
import numpy as np
from contextlib import ExitStack
import concourse.bass as bass
import concourse.mybir as mybir
from concourse.bass_utils import run_bass_kernel_spmd

F32 = mybir.dt.float32
BF16 = mybir.dt.bfloat16
AF = mybir.ActivationFunctionType
ALU = mybir.AluOpType

NCORES = 8
D = 2048
DC = 16
DFF = 5632
NCTX = 256
INC = 15936
EPS = 1e-6
GRID_W = 64
V_GQL, V_GKVL = 0, 4
V_GQBN, V_GQBP, V_GQBR, V_GKBN, V_GKBP, V_GKBR = 8, 9, 10, 11, 12, 13
V_GQC, V_GQCR, V_GKC, V_GKCR = 14, 15, 16, 17
V_BDWA, V_GLNA, V_BLNA = 18, 26, 34
V_WDWD = 42
V_WDWA = 66
NV = 66 + 248


class Res:
    __slots__ = ("name", "w", "rd")

    def __init__(self, name):
        self.name = name
        self.w = {}
        self.rd = {}


class K:
    ENG = ("pe", "act", "dve", "pool", "sp")

    def __init__(self, nc):
        self.nc = nc
        self.q = {e: [] for e in self.ENG}
        self.cnt = {}
        self.waited = {e: {} for e in self.ENG}
        self.nres = 0

    def res(self, name):
        self.nres += 1
        return Res(f"{name}_{self.nres}")

    def _deps(self, eng, reads, writes):
        need = {}
        for r in reads:
            for sk, v in r.w.items():
                need[sk] = max(need.get(sk, 0), v)
        for r in writes:
            for sk, v in r.w.items():
                need[sk] = max(need.get(sk, 0), v)
            for sk, v in r.rd.items():
                need[sk] = max(need.get(sk, 0), v)
        waits = []
        wd = self.waited[eng]
        for sk, v in need.items():
            if sk == eng and eng == "pe":
                continue
            if wd.get(sk, 0) >= v:
                continue
            wd[sk] = v
            waits.append((sk, v))
        return waits

    def op(self, eng, fn, reads=(), writes=(), inc=True, dma=None, sk=None):
        waits = self._deps(eng, reads, writes)
        tok = None
        if inc:
            if sk is not None:
                amt = 1
            elif dma is not None:
                sk, amt = "d_" + dma.name.rsplit("_", 1)[0], 16
            else:
                sk, amt = eng, 1
            self.cnt[sk] = self.cnt.get(sk, 0) + amt
            tok = (sk, amt, self.cnt[sk])
        self.q[eng].append((waits, fn, tok))
        if tok is not None:
            for r in reads:
                r.rd[tok[0]] = tok[2]
            for r in writes:
                r.w[tok[0]] = tok[2]
                r.rd = {}
        return tok

    def barrier(self):
        snap = dict(self.cnt)
        for e in self.ENG:
            waits = []
            for sk, v in snap.items():
                if sk == e and e == "pe":
                    continue
                if self.waited[e].get(sk, 0) >= v:
                    continue
                self.waited[e][sk] = v
                waits.append((sk, v))
            if waits:
                self.q[e].append((waits, None, None))

    def emit(self, es):
        nc = self.nc
        sems = {}
        for i, sk in enumerate(self.cnt.keys()):
            sems[sk] = es.enter_context(nc.semaphore(f"s{i}"))
        block = es.enter_context(nc.Block())

        def run(e, lst):
            for waits, fn, tok in lst:
                for sk, v in waits:
                    e.wait_ge(sems[sk], v)
                if fn is None:
                    continue
                ins = fn(e)
                if tok is not None:
                    ins.then_inc(sems[tok[0]], tok[1])

        @block.tensor
        def _(e):
            run(e, self.q["pe"])

        @block.scalar
        def _(e):
            run(e, self.q["act"])

        @block.vector
        def _(e):
            run(e, self.q["dve"])

        @block.gpsimd
        def _(e):
            run(e, self.q["pool"])

        @block.sync
        def _(e):
            run(e, self.q["sp"])


def build(NL, L, stop=99):
    T = NL + NCTX
    NK = 4 * NL + NCTX
    chunks = [(c * 512, 512, False) for c in range(NL // 512)] + [(NL, NCTX, True)]
    nc = bass.Bass("TRN2", target_bir_lowering=False)
    k = K(nc)
    es = ExitStack()

    def din(name, shape, dt=F32):
        return nc.dram_tensor(name, list(shape), dt, kind="ExternalInput")

    def dscr(name, shape, dt):
        return nc.dram_tensor(name, list(shape), dt), k.res(name)

    xin = din("xin", [NL, D])
    cin = din("cin", [NCTX, D])
    cvecT = din("cvecT", [128, DC * 3])
    wmod = din("wmod", [L * D, 2304])
    bmodT = din("bmodT", [128, L * 18])
    vecs_in = din("vecs", [128, L * NV])
    sel_in = din("sel", [128, 18])
    ident_in = din("ident", [128, 128])
    cosB_in, sinB_in = din("cosB", [64, T]), din("sinB", [64, T])
    cosC_in, sinC_in = din("cosC", [128, T]), din("sinC", [128, T])
    yout = nc.dram_tensor("yout", [NL, D], F32, kind="ExternalOutput")
    r_yout = k.res("yout")
    WSPEC = {
        "f1i": (256, 2 * DFF), "f1o": (DFF, 256), "win": (256, INC), "wukv": (64, 2048),
        "wuq": (64, 1536), "woa": (128, D), "wob": (128, D), "woc": (128, D), "wod": (128, D),
        "wo": (256, D), "f2i": (256, 2 * DFF), "f2o": (DFF, 256),
    }
    w_in_ext, w_b, w_g, r_wg = {}, {}, {}, {}
    SMALL = ("wukv", "wuq", "woa", "wob", "woc", "wod", "wo")
    BOFF, BS = {}, 0
    for nm in SMALL:
        BOFF[nm] = BS
        BS += WSPEC[nm][0] * WSPEC[nm][1]
    for nm, (R, M) in WSPEC.items():
        w_in_ext[nm] = din("w_" + nm, [L * R, M])
    for l in range(L):
        for nm, (R, M) in WSPEC.items():
            if nm in SMALL:
                continue
            w_b[nm, l] = nc.dram_tensor(f"wb_{nm}{l}", [R, M], BF16)
            w_g[nm, l] = nc.dram_tensor(f"wg_{nm}{l}", [NCORES * R, M], BF16)
            r_wg[nm, l] = k.res(f"wg_{nm}{l}")
        w_b["blob", l] = nc.dram_tensor(f"wb_blob{l}", [1, BS], BF16)
        w_g["blob", l] = nc.dram_tensor(f"wg_blob{l}", [NCORES, BS], BF16)
        r_wg["blob", l] = k.res(f"wg_blob{l}")

    hT, r_hT = dscr("hT", [DC, 128, T], F32)
    xmT, r_xm = dscr("xmT", [DC, 128, T], BF16)
    def xlay(n):
        o = {}
        p = 0
        for nm, sz in (("kbn", 8 * 128 * n), ("kbp", 8 * 64 * n), ("vb", n * 1024), ("kc", 2 * 128 * n),
                       ("vc", n * 256), ("ha", 8 * 128 * 30), ("hu", 8 * 128 * 2)):
            o[nm] = p
            p += sz
        return o, p
    XO, XS = xlay(NL)
    CO, CS = xlay(NCTX)
    xown, r_xown = dscr("xown", [1, XS], BF16)
    xall, r_xall = dscr("xall", [NCORES, XS], BF16)
    cown, r_cown = dscr("cown", [1, CS], BF16)
    qb, r_qb = dscr("qb", [8, 192, T], BF16)
    qc, r_qc = dscr("qc", [8, 128, T], BF16)
    att, r_att = dscr("att", [16, 128, T], BF16)
    aT, r_aT = dscr("aT", [8, 128, T], BF16)
    u2T, r_u2 = dscr("u2T", [8, 128, T], BF16)
    bgT, r_bg = dscr("bgT", [8, 128, T], BF16)
    laT, r_la = dscr("laT", [8, 128, T], BF16)
    ydT, r_yd = dscr("ydT", [8, 128, T], BF16)
    modb = nc.dram_tensor("modb", [128, 4096], F32)
    r_modb = k.res("modb")
    modg = nc.dram_tensor("modg", [NCORES * 128, 4096], F32)
    r_modg = k.res("modg")

    def sb(st, name, shape, dt):
        t = st.enter_context(nc.sbuf_tensor(f"{name}_s{k.nres + 1}", list(shape), dt))
        return t, k.res(name)

    def ps(st, name):
        t = st.enter_context(nc.psum_tensor(f"{name}_p{k.nres + 1}", [128, 512], F32))
        return t, k.res(name)

    def load(dst, src, rdst, rsrc=None, eng="sp"):
        k.op(eng, lambda e: e.dma_start(out=dst, in_=src), reads=[rsrc] if rsrc else [], writes=[rdst], dma=rdst)

    def store(dst, src, rdst, rsrc, eng="pool"):
        k.op(eng, lambda e: e.dma_start(out=dst, in_=src), reads=[rsrc], writes=[rdst], dma=rdst)

    def mm(out, pairs, reads, wres):
        n = len(pairs)
        for i, (l_, r_) in enumerate(pairs):
            st, sp_ = (i == 0), (i == n - 1)
            k.op("pe", lambda e, o=out, a=l_, b=r_, s=st, p=sp_: e.matmul(o, a, b, start=s, stop=p),
                 reads=reads, writes=[wres], inc=sp_)

    def tr(out, in_, ident, reads, wres):
        k.op("pe", lambda e: e.transpose(out, in_, ident), reads=reads, writes=[wres])

    def act(out, in_, func, reads, writes, bias=0.0, scale=1.0, eng="act"):
        k.op(eng, lambda e: e.activation(out, in_, func, bias=bias, scale=scale), reads=reads, writes=writes)

    def tt(out, a, b, op, reads, writes, eng="dve"):
        k.op(eng, lambda e: e.tensor_tensor(out, a, b, op), reads=reads, writes=writes)

    def ts(out, a, s1, s2, op0, op1, reads, writes, eng="dve"):
        if op1 is None:
            k.op(eng, lambda e: e.tensor_scalar(out, a, s1, None, op0), reads=reads, writes=writes)
        else:
            k.op(eng, lambda e: e.tensor_scalar(out, a, s1, s2, op0, op1), reads=reads, writes=writes)

    def stt(out, a, s, b, op0, op1, reads, writes, eng="dve"):
        k.op(eng, lambda e: e.scalar_tensor_tensor(out, a, s, b, op0, op1), reads=reads, writes=writes)

    def cp(out, in_, reads, writes, eng="dve"):
        if eng == "act":
            k.op(eng, lambda e: e.copy(out, in_), reads=reads, writes=writes)
        else:
            k.op(eng, lambda e: e.tensor_copy(out, in_), reads=reads, writes=writes)

    ident, r_ident = sb(es, "ident", [128, 128], F32)
    ones, r_ones = sb(es, "ones", [128, 128], F32)
    onesb, r_onesb = sb(es, "onesb", [128, 128], BF16)
    identb, r_identb = sb(es, "identb", [128, 128], BF16)
    modL, r_modL = sb(es, "modL", [128, L, 144], F32)
    modC, r_modC = sb(es, "modC", [128, L, 144], F32)
    vecs, r_vecs = sb(es, "vecs", [128, L * NV], F32)
    sel, r_sel = sb(es, "sel", [128, 18], F32)
    PB = [ps(es, f"pb{i}") for i in range(8)]
    r_const = [r_ident, r_ones, r_onesb, r_identb, r_modL, r_modC, r_vecs, r_sel]

    load(ident[:, :], ident_in[:, :], r_ident)
    load(vecs[:, :], vecs_in[:, :], r_vecs)
    load(sel[:, :], sel_in[:, :], r_sel)
    k.op("dve", lambda e: e.memset(ones[:, :], 1.0), writes=[r_ones])
    k.op("dve", lambda e: e.memset(onesb[:, :], 1.0), writes=[r_onesb])
    cp(identb[:, :], ident[:, :], [r_ident], [r_identb])

    def vcol(l, c, n=1, rows=128):
        return vecs[0:rows, l * NV + c: l * NV + c + n]

    with ExitStack() as st:
        CW = 4096
        stg = [sb(st, f"stg{i}", [128, CW], F32) for i in range(2)]
        stb = [sb(st, f"stb{i}", [128, CW], BF16) for i in range(2)]
        it = 0
        for l in range(L):
            r_blob = k.res(f"wb_blob{l}")
            for nm, (R, M) in WSPEC.items():
                small = nm in SMALL
                r_wb = r_blob if small else k.res(f"wb_{nm}{l}")
                tot = R * M // 128
                src = w_in_ext[nm][l * R:(l + 1) * R, :].rearrange("a b -> (a b)").rearrange("(p n) -> p n", p=128)
                if small:
                    dst = w_b["blob", l][0, BOFF[nm]:BOFF[nm] + R * M].rearrange("(p n) -> p n", p=128)
                else:
                    dst = w_b[nm, l].ap().rearrange("a b -> (a b)").rearrange("(p n) -> p n", p=128)
                for c0 in range(0, tot, CW):
                    cw = min(CW, tot - c0)
                    (sg, rsg), (sbb, rsb) = stg[it % 2], stb[it % 2]
                    load(sg[:, 0:cw], src[:, c0:c0 + cw], rsg)
                    cp(sbb[:, 0:cw], sg[:, 0:cw], [rsg], [rsb], eng=("dve", "act")[it % 2])
                    store(dst[:, c0:c0 + cw], sbb[:, 0:cw], r_wb, rsb, eng="sp")
                    it += 1
                if not small:
                    k.op("pool", lambda e, a=w_b[nm, l], b=w_g[nm, l]: e.collective_compute(
                        "AllGather", ALU.bypass, replica_groups=[list(range(NCORES))],
                        ins=[a.ap().opt()], outs=[b.ap().opt()]), reads=[r_wb], writes=[r_wg[nm, l]], sk="cc")
            k.op("pool", lambda e, a=w_b["blob", l], b=w_g["blob", l]: e.collective_compute(
                "AllGather", ALU.bypass, replica_groups=[list(range(NCORES))],
                ins=[a.ap().opt()], outs=[b.ap().opt()]), reads=[r_blob], writes=[r_wg["blob", l]], sk="cc")
    k.barrier()

    def load_w(dst_tile, rdst, nm, l, c0, ncol, d0=0):
        if nm not in SMALL:
            src = w_g[nm, l][:, c0:c0 + ncol].rearrange("(kc p) m -> p kc m", p=128)
            load(dst_tile[:, :, d0:d0 + ncol], src, rdst, r_wg[nm, l])
            return
        R, M = WSPEC[nm]
        g = w_g["blob", l][:, BOFF[nm]:BOFF[nm] + R * M].rearrange("r (k m) -> r k m", m=M)[:, :, c0:c0 + ncol]
        rr = r_wg["blob", l]
        if R == 128:
            load(dst_tile[:, :, d0:d0 + ncol], g.rearrange("r p m -> p r m"), rdst, rr)
        elif R == 256:
            for h in range(2):
                dv = dst_tile[:, :, d0:d0 + ncol].rearrange("p (r h) m -> p h r m", h=2)[:, h]
                load(dv, g[:, h * 128:(h + 1) * 128, :].rearrange("r p m -> p r m"), rdst, rr)
        else:
            for par in range(2):
                gv = g.rearrange("(kc two) p m -> two p kc m", two=2)[par]
                load(dst_tile[par * 64:(par + 1) * 64, :, d0:d0 + ncol], gv, rdst, rr)

    with ExitStack() as st:
      if stop >= 2:
            cv, r_cv = sb(st, "cv", [128, DC * 3], F32)
            scv, r_scv = sb(st, "scv", [128, DC, 3], F32)
            bm, r_bm = sb(st, "bm", [128, L * 18], F32)
            mfull, r_mpart = sb(st, "mpart", [128, 4096], F32)
            k.op("dve", lambda e: e.memset(mfull[:, :], 0.0), writes=[r_mpart])
            mpart = mfull[:, 0:L * 54].rearrange("p (l c i) -> p l c i", l=L, c=18)
            mall, r_mall = sb(st, "mall", [128, NCORES, L * 54], F32)
            wmb = [sb(st, f"wmb{i}", [128, DC, 384], F32) for i in range(2)]
            load(cv[:, :], cvecT[:, :], r_cv)
            load(bm[:, :], bmodT[:, :], r_bm)
            act(scv[:, :, :].rearrange("p a b -> p (a b)"), cv[:, :], AF.Silu, [r_cv], [r_scv])
            it = 0
            for l in range(L):
                pbt, rpb = PB[l % 2]
                for blk in range(6):
                    wt, rwt = wmb[it % 2]
                    it += 1
                    load(wt[:, :, :], wmod[l * D:(l + 1) * D, blk * 384:(blk + 1) * 384].rearrange("(kc p) m -> p kc m", p=128), rwt)
                    for cc in range(3):
                        ch = blk * 3 + cc
                        mm(pbt[:, ch * 3:ch * 3 + 3], [(wt[:, kc, cc * 128:(cc + 1) * 128], scv[:, kc, :]) for kc in range(DC)],
                           [rwt, r_scv], rpb)
                for i in range(3):
                    tt(mpart[:, l, :, i], pbt[:, 0:54].rearrange("p (c i) -> p c i", i=3)[:, :, i], bm[:, l * 18:(l + 1) * 18],
                       ALU.add, [rpb, r_bm], [r_mpart])
            store(modb[:, :], mfull[:, :], r_modb, r_mpart)
            k.op("pool", lambda e: e.collective_compute("AllGather", ALU.bypass, replica_groups=[list(range(NCORES))],
                 ins=[modb.ap().opt()], outs=[modg.ap().opt()]), reads=[r_modb], writes=[r_modg], sk="cc")
            load(mall[:, :, :], modg.ap().rearrange("(r p) n -> p r n", p=128)[:, :, 0:L * 54], r_mall, r_modg)
            for l in range(L):
                mv = mall[:, :, l * 54:(l + 1) * 54].rearrange("p r (c i) -> p r c i", i=3)
                oL = modL[:, l, :].rearrange("p (r c) -> p r c", r=NCORES)
                oC = modC[:, l, :].rearrange("p (r c) -> p r c", r=NCORES)
                ts(oL, mv[:, :, :, 0], sel[:, 0:1], None, ALU.mult, None, [r_mall, r_sel], [r_modL])
                stt(oL, mv[:, :, :, 1], sel[:, 1:2], oL, ALU.mult, ALU.add, [r_mall, r_sel, r_modL], [r_modL])
                cp(oC, mv[:, :, :, 2], [r_mall], [r_modC])
                for mt, rm in ((modL, r_modL), (modC, r_modC)):
                    for n in (1, 4, 7):
                        ts(mt[:, l, n * 16:(n + 1) * 16], mt[:, l, n * 16:(n + 1) * 16], 1.0, None, ALU.add, None, [rm], [rm])
                    for n in (2, 8):
                        ts(mt[:, l, n * 16:(n + 1) * 16], mt[:, l, n * 16:(n + 1) * 16], 0.5, None, ALU.mult, None, [rm], [rm])
    k.barrier()

    def mcol(isctx, l, n, dc):
        m = modC if isctx else modL
        return m[:, l, n * 16 + dc:n * 16 + dc + 1]

    with ExitStack() as st:
        xt = [sb(st, f"xt{i}", [128, D], F32) for i in range(2)]
        xo = [sb(st, f"xo{i}", [128, DC, 128], F32) for i in range(2)]
        ntile = T // 128
        for ti in range(ntile if stop >= 3 else 0):
            (xa, rxa), (xb_, rxb) = xt[ti % 2], xo[ti % 2]
            srcrows = xin[ti * 128:(ti + 1) * 128, :] if ti < NL // 128 else cin[(ti - NL // 128) * 128:(ti - NL // 128 + 1) * 128, :]
            load(xa[:, :], srcrows, rxa)
            for g in range(4):
                pbt, rpb = PB[(ti * 4 + g) % 8]
                for j in range(4):
                    dc = g * 4 + j
                    tr(pbt[:, j * 128:(j + 1) * 128], xa[:, dc * 128:(dc + 1) * 128], ident[:, :], [rxa, r_ident], rpb)
                cp(xb_[:, g * 4:(g + 1) * 4, :], pbt[:, :].rearrange("p (j t) -> p j t", j=4), [rpb], [rxb],
                   eng=("dve", "act")[g % 2])
            store(hT.ap().rearrange("c p t -> p c t")[:, :, ti * 128:(ti + 1) * 128], xb_[:, :, :], r_hT, rxb)
    k.barrier()

    def rms_stats(src, rsrc, nch, n, rows, sq, rstd, rrstd, bank, dsz, eps_, mult_):
        pbt, rpb = bank
        for i in range(nch):
            (sq_t, rsq) = sq[i % len(sq)]
            act(sq_t[0:rows, 0:n], src[i], AF.Square, [rsrc], [rsq])
            k.op("pe", lambda e, a=sq_t, i=i, pbt=pbt, n=n, rows=rows, nch=nch: e.matmul(pbt[:, 0:n], ones[0:rows, :], a[0:rows, 0:n], start=(i == 0), stop=(i == nch - 1)),
                 reads=[rsq, r_ones], writes=[rpb], inc=True)
        ts(rstd[:, 0:n], pbt[:, 0:n], mult_, eps_, ALU.mult, ALU.add, [rpb], [rrstd])
        act(rstd[:, 0:n], rstd[:, 0:n], AF.Sqrt, [rrstd], [rrstd])
        k.op("dve", lambda e, rstd=rstd, n=n: e.reciprocal(rstd[:, 0:n], rstd[:, 0:n]), reads=[rrstd], writes=[rrstd])

    def ffn_phase(l, which):
        wi, wo_ = ("f1i", "f1o") if which == 1 else ("f2i", "f2o")
        n_sh, n_sc, n_g = (0, 1, 2) if which == 1 else (6, 7, 8)
        with ExitStack() as st:
            h, r_h = sb(st, "h", [128, DC, 512], F32)
            xn, r_xn = sb(st, "xn", [128, DC, 512], BF16)
            hid, r_hid = sb(st, "hid", [128, 44, 512], BF16)
            sq = [sb(st, f"sq{i}", [128, 512], F32) for i in range(2)]
            sa = [sb(st, f"sa{i}", [128, 512], F32) for i in range(2)]
            tq = [sb(st, f"tq{i}", [128, 512], F32) for i in range(2)]
            rstd, r_rstd = sb(st, "rstd", [128, 512], F32)
            wa = [sb(st, f"wa{i}", [128, DC, 256], BF16) for i in range(2)]
            wb_ = [sb(st, f"wbt{i}", [128, DC, 256], BF16) for i in range(2)]
            wot = [sb(st, f"wot{i}", [128, 22, 256], BF16) for i in range(2)]
            wit = 0
            wot_i = 0
            for (t0, n, isctx) in chunks:
                if isctx and l == L - 1 and which == 2:
                    continue
                load(h[:, :, 0:n], hT.ap().rearrange("c p t -> p c t")[:, :, t0:t0 + n], r_h, r_hT)

                def norm_mod(n_sh, n_sc):
                    rms_stats([h[:, dc, 0:n] for dc in range(DC)], r_h, DC, n, 128, sq, rstd, r_rstd, PB[6], D, EPS, 1.0 / D)
                    for dc in range(DC):
                        tqt, rtq = tq[dc % 2]
                        stt(tqt[:, 0:n], h[:, dc, 0:n], mcol(isctx, l, n_sc, dc), rstd[:, 0:n], ALU.mult, ALU.mult,
                            [r_h, r_rstd] + r_const, [rtq])
                        act(xn[:, dc, 0:n], tqt[:, 0:n], AF.Identity, [rtq] + r_const, [r_xn], bias=mcol(isctx, l, n_sh, dc))
                norm_mod(n_sh, n_sc)
                for blk in range(22):
                    (wat, rwa), (wbt, rwb) = wa[wit % 2], wb_[wit % 2]
                    wit += 1
                    load_w(wat, rwa, wi, l, blk * 256, 256)
                    load_w(wbt, rwb, wi, l, DFF + blk * 256, 256)
                    for jj in range(2):
                        j = blk * 2 + jj
                        pa, rpa = PB[j % 2]
                        pbk, rpbk = PB[2 + j % 2]
                        mm(pa[:, 0:n], [(wat[:, kc, jj * 128:(jj + 1) * 128], xn[:, kc, 0:n]) for kc in range(DC)], [rwa, r_xn], rpa)
                        mm(pbk[:, 0:n], [(wbt[:, kc, jj * 128:(jj + 1) * 128], xn[:, kc, 0:n]) for kc in range(DC)], [rwb, r_xn], rpbk)
                        sat, rsa = sa[j % 2]
                        act(sat[:, 0:n], pa[:, 0:n], AF.Silu, [rpa], [rsa])
                        tt(hid[:, j, 0:n], sat[:, 0:n], pbk[:, 0:n], ALU.mult, [rsa, rpbk], [r_hid])
                for r in range(NCORES):
                    for mh in range(2):
                        m = r * 2 + mh
                        po, rpo = PB[4 + m % 2]
                        for kh in range(2):
                            if mh == 0:
                                wt, rwt = wot[(wot_i + kh) % 2]
                                src = w_g[wo_, l][r * DFF + kh * 2816:r * DFF + (kh + 1) * 2816, :].rearrange("(kc p) m -> p kc m", p=128)
                                load(wt[:, :, :], src, rwt, r_wg[wo_, l])
                            wt, rwt = wot[(wot_i + kh) % 2]
                            for kc in range(22):
                                kk = kh * 22 + kc
                                k.op("pe", lambda e, o=po, a=wt, b=hid, kc=kc, kk=kk, mh=mh, n=n: e.matmul(
                                    o[:, 0:n], a[:, kc, mh * 128:(mh + 1) * 128], b[:, kk, 0:n], start=(kk == 0), stop=(kk == 43)),
                                    reads=[wot[0][1], wot[1][1], r_hid], writes=[rpo], inc=(kk == 43))
                        stt(h[:, m, 0:n], po[:, 0:n], mcol(isctx, l, n_g, m), h[:, m, 0:n], ALU.mult, ALU.add,
                            [rpo, r_h] + r_const, [r_h])
                    wot_i += 2
                store(hT.ap().rearrange("c p t -> p c t")[:, :, t0:t0 + n], h[:, :, 0:n], r_hT, r_h)
                if which == 1:
                    norm_mod(3, 4)
                    store(xmT.ap().rearrange("c p t -> p c t")[:, :, t0:t0 + n], xn[:, :, 0:n], r_xm, r_xn)
        k.barrier()

    def xo_view(buf, off_map, key, n):
        return buf, off_map[key]

    def proj_phase(l):
        last = (l == L - 1)
        with ExitStack() as st:
            xm, r_xmS = sb(st, "xm", [128, DC, 512], BF16)
            wun = [sb(st, f"wun{i}", [128, DC, 512], BF16) for i in range(2)]
            wukv_t, r_wukv = sb(st, "wukv_t", [128, 4, 2048], BF16)
            wuq_t, r_wuq = sb(st, "wuq_t", [128, 4, 1536], BF16)
            wuqr_t, r_wuqr = sb(st, "wuqr_t", [128, 4, 512], BF16)
            lat, r_lat = sb(st, "lat", [128, 4, 512], F32)
            latn, r_latn = sb(st, "latn", [128, 4, 512], BF16)
            kr, r_kr = sb(st, "kr", [64, 512], F32)
            krr, r_krr = sb(st, "krr", [64, 512], F32)
            tb = {nm: sb(st, nm, [128, 512], F32) for nm in ("cb", "sbt", "cc", "sc")}
            sq = [sb(st, f"sq{i}", [128, 512], F32) for i in range(2)]
            ev = [sb(st, f"ev{i}", [128, 512], F32) for i in range(6)]
            ob = [sb(st, f"ob{i}", [128, 512], BF16) for i in range(4)]
            rstd, r_rstd = sb(st, "rstd", [128, 512], F32)
            load_w(wukv_t, r_wukv, "wukv", l, 0, 2048)
            load_w(wuq_t, r_wuq, "wuq", l, 0, 1536)
            for hd in range(8):
                load_w(wuqr_t, r_wuqr, "wuq", l, hd * 192 + 160, 32, d0=hd * 64)
                load_w(wuqr_t, r_wuqr, "wuq", l, hd * 192 + 128, 32, d0=hd * 64 + 32)
            state = {"u": 0, "e": 0, "o": 0, "b": 0, "s": 0}

            def unit(pieces):
                wt, rwt = wun[state["u"] % 2]
                state["u"] += 1
                d0 = 0
                for (c0, ncol) in pieces:
                    load_w(wt, rwt, "win", l, c0, ncol, d0=d0)
                    d0 += ncol
                return wt, rwt

            def nxt(lst, key):
                x = lst[state[key] % len(lst)]
                state[key] += 1
                return x

            def bank():
                b = PB[state["b"] % 6]
                state["b"] += 1
                return b

            for (t0, n, isctx) in chunks:
                only_kv = isctx and last
                xbuf, xoff, r_xb, nt = (cown, CO, r_cown, NCTX) if isctx else (xown, XO, r_xown, NL)
                tl = t0 - (NL if isctx else 0)

                def xdst(key, rows, hd, rowlen=None):
                    base = xoff[key] + hd * rows * nt
                    return xbuf[0, base:base + rows * nt].rearrange("(p t) -> p t", t=nt)[:, tl:tl + n]

                load(xm[:, :, 0:n], xmT.ap().rearrange("c p t -> p c t")[:, :, t0:t0 + n], r_xmS, r_xm)
                for nm, src, rows in (("cb", cosB_in, 64), ("sbt", sinB_in, 64), ("cc", cosC_in, 128), ("sc", sinC_in, 128)):
                    load(tb[nm][0][0:rows, 0:n], src[:, t0:t0 + n], tb[nm][1])

                def fm(wt, rwt, off, M, dst=None):
                    pbt, rpb = bank()
                    mm(pbt[0:M, 0:n], [(wt[:, kc, off:off + M], xm[:, kc, 0:n]) for kc in range(DC)], [rwt, r_xmS], rpb)
                    return pbt, rpb

                def rope_norm(x_ps, rx, xr_ps, rxr, rows, gcol, grcol, cosn, sinn, dst, extra_sq=None):
                    (e1, re1), (e2, re2) = nxt(ev, "e"), nxt(ev, "e")
                    o1, ro1 = nxt(ob, "o")
                    ct, rct = tb[cosn]
                    stt_, rst_ = tb[sinn]
                    stt(e1[0:rows, 0:n], x_ps, vcol(l, gcol, rows=rows), ct[0:rows, 0:n], ALU.mult, ALU.mult, [rx, rct] + r_const, [re1])
                    stt(e2[0:rows, 0:n], xr_ps, vcol(l, grcol, rows=rows), stt_[0:rows, 0:n], ALU.mult, ALU.mult, [rxr, rst_] + r_const, [re2])
                    tt(e1[0:rows, 0:n], e1[0:rows, 0:n], e2[0:rows, 0:n], ALU.add, [re1, re2], [re1])
                    tt(o1[0:rows, 0:n], e1[0:rows, 0:n], rstd[0:rows, 0:n], ALU.mult, [re1, r_rstd], [ro1])
                    store(dst, o1[0:rows, 0:n], r_xb if dst_is_x[0] else dst_res[0], ro1)

                dst_is_x = [True]
                dst_res = [None]

                def head_stats(parts, dsz, mult_, eps_):
                    pbt, rpb = PB[6]
                    np_ = len(parts)
                    for i, (ap_, rows, rr) in enumerate(parts):
                        sq_t, rsq = nxt(sq, "s")
                        act(sq_t[0:rows, 0:n], ap_, AF.Square, [rr], [rsq])
                        k.op("pe", lambda e, a=sq_t, rows=rows, i=i, n=n, pbt=pbt, np_=np_: e.matmul(pbt[:, 0:n], ones[0:rows, :], a[0:rows, 0:n],
                             start=(i == 0), stop=(i == np_ - 1)), reads=[rsq, r_ones], writes=[rpb], inc=True)
                    ts(rstd[:, 0:n], pbt[:, 0:n], mult_, eps_, ALU.mult, ALU.add, [rpb], [r_rstd])
                    act(rstd[:, 0:n], rstd[:, 0:n], AF.Sqrt, [r_rstd], [r_rstd])
                    k.op("dve", lambda e, rstd=rstd, n=n: e.reciprocal(rstd[:, 0:n], rstd[:, 0:n]), reads=[r_rstd], writes=[r_rstd])

                wt, rwt = unit([(0, 512)])
                for j in range(4):
                    pbt, rpb = fm(wt, rwt, j * 128, 128)
                    cp(lat[:, j, 0:n], pbt[:, 0:n], [rpb], [r_lat], eng="act")
                rms_stats([lat[:, j, 0:n] for j in range(4)], r_lat, 4, n, 128, sq, rstd, r_rstd, PB[6], 512, EPS, 1.0 / 512)
                for j in range(4):
                    stt(latn[:, j, 0:n], lat[:, j, 0:n], vcol(l, V_GKVL + j), rstd[:, 0:n], ALU.mult, ALU.mult,
                        [r_lat, r_rstd] + r_const, [r_latn])
                wt, rwt = unit([(512, 64), (544, 32), (512, 32), (576, 256)])
                pbt, rpb = fm(wt, rwt, 0, 64)
                cp(kr[:, 0:n], pbt[0:64, 0:n], [rpb], [r_kr], eng="act")
                pbt, rpb = fm(wt, rwt, 64, 64)
                cp(krr[:, 0:n], pbt[0:64, 0:n], [rpb], [r_krr], eng="act")
                kc_ps = [fm(wt, rwt, 128 + hh * 128, 128) for hh in range(2)]
                kc_sb = []
                for hh in range(2):
                    e1, re1 = nxt(ev, "e")
                    cp(e1[:, 0:n], kc_ps[hh][0][:, 0:n], [kc_ps[hh][1]], [re1])
                    kc_sb.append((e1, re1))
                wt, rwt = unit([(576 + 64, 64), (576, 64), (704 + 64, 64), (704, 64), (832, 256)])
                for hh in range(2):
                    pr, rpr = fm(wt, rwt, hh * 128, 128)
                    e1, re1 = kc_sb[hh]
                    head_stats([(e1[:, 0:n], 128, re1)], 128, 1.0 / 128, EPS)
                    rope_norm(e1[:, 0:n], re1, pr[:, 0:n], rpr, 128, V_GKC, V_GKCR, "cc", "sc", xdst("kc", 128, hh))
                for tt_ in range(n // 128):
                    pbt, rpb = bank()
                    mm(pbt[:, 0:256], [(xm[:, kc, tt_ * 128:(tt_ + 1) * 128], wt[:, kc, 256:512]) for kc in range(DC)], [rwt, r_xmS], rpb)
                    o1, ro1 = nxt(ob, "o")
                    cp(o1[:, 0:256], pbt[:, 0:256], [rpb], [ro1], eng="act")
                    base = xoff["vc"] + (tl + tt_ * 128) * 256
                    store(xbuf[0, base:base + 128 * 256].rearrange("(p d) -> p d", d=256), o1[:, 0:256], r_xb, ro1)
                for tt_ in range(n // 128):
                    for half in range(2):
                        pbt, rpb = bank()
                        for h4 in range(4):
                            hdv = half * 4 + h4
                            mm(pbt[:, h4 * 128:(h4 + 1) * 128],
                               [(latn[:, kc, tt_ * 128:(tt_ + 1) * 128], wukv_t[:, kc, hdv * 256 + 128:hdv * 256 + 256]) for kc in range(4)],
                               [r_wukv, r_latn], rpb)
                        o1, ro1 = nxt(ob, "o")
                        cp(o1[:, 0:512], pbt[:, 0:512], [rpb], [ro1], eng="act")
                        base = xoff["vb"] + (tl + tt_ * 128) * 1024
                        dstv = xbuf[0, base:base + 128 * 1024].rearrange("(p d) -> p d", d=1024)[:, half * 512:(half + 1) * 512]
                        store(dstv, o1[:, 0:512], r_xb, ro1)
                for hd in range(8):
                    pbt, rpb = bank()
                    mm(pbt[:, 0:n], [(wukv_t[:, kc, hd * 256:hd * 256 + 128], latn[:, kc, 0:n]) for kc in range(4)], [r_wukv, r_latn], rpb)
                    e1, re1 = nxt(ev, "e")
                    cp(e1[:, 0:n], pbt[:, 0:n], [rpb], [re1], eng="act")
                    head_stats([(e1[:, 0:n], 128, re1), (kr[:, 0:n], 64, r_kr)], 192, 1.0 / 192, EPS)
                    o1, ro1 = nxt(ob, "o")
                    stt(o1[:, 0:n], e1[:, 0:n], vcol(l, V_GKBN), rstd[:, 0:n], ALU.mult, ALU.mult, [re1, r_rstd] + r_const, [ro1])
                    store(xdst("kbn", 128, hd), o1[:, 0:n], r_xb, ro1)
                    rope_norm(kr[:, 0:n], r_kr, krr[:, 0:n], r_krr, 64, V_GKBP, V_GKBR, "cb", "sbt", xdst("kbp", 64, hd))
                if only_kv:
                    continue
                wt, rwt = unit([(1088, 512)])
                for j in range(4):
                    pbt, rpb = fm(wt, rwt, j * 128, 128)
                    cp(lat[:, j, 0:n], pbt[:, 0:n], [rpb], [r_lat], eng="act")
                rms_stats([lat[:, j, 0:n] for j in range(4)], r_lat, 4, n, 128, sq, rstd, r_rstd, PB[6], 512, EPS, 1.0 / 512)
                for j in range(4):
                    stt(latn[:, j, 0:n], lat[:, j, 0:n], vcol(l, V_GQL + j), rstd[:, 0:n], ALU.mult, ALU.mult,
                        [r_lat, r_rstd] + r_const, [r_latn])
                dst_is_x[0] = False
                dst_res[0] = r_qb
                for hd in range(8):
                    pn, rpn = bank()
                    mm(pn[:, 0:n], [(wuq_t[:, kc, hd * 192:hd * 192 + 128], latn[:, kc, 0:n]) for kc in range(4)], [r_wuq, r_latn], rpn)
                    pp, rpp = bank()
                    mm(pp[0:64, 0:n], [(wuq_t[:, kc, hd * 192 + 128:hd * 192 + 192], latn[:, kc, 0:n]) for kc in range(4)], [r_wuq, r_latn], rpp)
                    pr, rpr = bank()
                    mm(pr[0:64, 0:n], [(wuqr_t[:, kc, hd * 64:hd * 64 + 64], latn[:, kc, 0:n]) for kc in range(4)], [r_wuqr, r_latn], rpr)
                    e1, re1 = nxt(ev, "e")
                    e2, re2 = nxt(ev, "e")
                    e3, re3 = nxt(ev, "e")
                    cp(e1[:, 0:n], pn[:, 0:n], [rpn], [re1], eng="act")
                    cp(e2[0:64, 0:n], pp[0:64, 0:n], [rpp], [re2], eng="act")
                    cp(e3[0:64, 0:n], pr[0:64, 0:n], [rpr], [re3], eng="act")
                    head_stats([(e1[:, 0:n], 128, re1), (e2[0:64, 0:n], 64, re2)], 192, 1.0, 192 * EPS)
                    o1, ro1 = nxt(ob, "o")
                    stt(o1[:, 0:n], e1[:, 0:n], vcol(l, V_GQBN), rstd[:, 0:n], ALU.mult, ALU.mult, [re1, r_rstd] + r_const, [ro1])
                    store(qb[hd, 0:128, t0:t0 + n], o1[:, 0:n], r_qb, ro1)
                    rope_norm(e2[0:64, 0:n], re2, e3[0:64, 0:n], re3, 64, V_GQBP, V_GQBR, "cb", "sbt", qb[hd, 128:192, t0:t0 + n])
                dst_res[0] = r_qc
                for g in range(2):
                    wt, rwt = unit([(1600 + g * 512, 512)])
                    wt2, rwt2 = unit([p for hh in range(4) for p in ((1600 + (g * 4 + hh) * 128 + 64, 64), (1600 + (g * 4 + hh) * 128, 64))])
                    for hh in range(4):
                        hd = g * 4 + hh
                        px, rpx = fm(wt, rwt, hh * 128, 128)
                        pr, rpr = fm(wt2, rwt2, hh * 128, 128)
                        e1, re1 = nxt(ev, "e")
                        cp(e1[:, 0:n], px[:, 0:n], [rpx], [re1], eng="act")
                        head_stats([(e1[:, 0:n], 128, re1)], 128, 1.0, 128 * EPS)
                        rope_norm(e1[:, 0:n], re1, pr[:, 0:n], rpr, 128, V_GQC, V_GQCR, "cc", "sc", qc[hd, :, t0:t0 + n])
                dst_is_x[0] = True
                for g in range(4):
                    wt, rwt = unit([(2624 + g * 256, 256), (2624 + 1024 + g * 256, 256)])
                    for jj in range(2):
                        i = g * 2 + jj
                        pa, rpa = fm(wt, rwt, jj * 128, 128)
                        pg, rpg = fm(wt, rwt, 256 + jj * 128, 128)
                        e1, re1 = nxt(ev, "e")
                        act(e1[:, 0:n], pg[:, 0:n], AF.Sigmoid, [rpg], [re1])
                        o1, ro1 = nxt(ob, "o")
                        tt(o1[:, 0:n], e1[:, 0:n], pa[:, 0:n], ALU.mult, [re1, rpa], [ro1])
                        store(aT[i, :, t0:t0 + n], o1[:, 0:n], r_aT, ro1)
                for g in range(2):
                    wt, rwt = unit([(4672 + g * 512, 512)])
                    for jj in range(4):
                        i = g * 4 + jj
                        pb_, rpb_ = fm(wt, rwt, jj * 128, 128)
                        o1, ro1 = nxt(ob, "o")
                        cp(o1[:, 0:n], pb_[:, 0:n], [rpb_], [ro1], eng="act")
                        store(bgT[i, :, t0:t0 + n], o1[:, 0:n], r_bg, ro1)
                for g in range(4):
                    wt, rwt = unit([(4672 + 1024 + g * 256, 256), (4672 + 2048 + g * 256, 256)])
                    for jj in range(2):
                        i = g * 2 + jj
                        pc_, rpc_ = fm(wt, rwt, jj * 128, 128)
                        ph_, rph_ = fm(wt, rwt, 256 + jj * 128, 128)
                        e1, re1 = nxt(ev, "e")
                        cp(e1[:, 0:n], pc_[:, 0:n], [rpc_], [re1], eng="act")
                        o1, ro1 = nxt(ob, "o")
                        tt(o1[:, 0:n], e1[:, 0:n], ph_[:, 0:n], ALU.mult, [re1, rph_], [ro1])
                        store(u2T[i, :, t0:t0 + n], o1[:, 0:n], r_u2, ro1)
            if os.environ.get("KNOCC"):
                k.barrier()
                return
            ha = xown[0, XO["ha"]:XO["ha"] + 8 * 128 * 30].rearrange("(c p s) -> c p s", c=8, p=128)
            hu = xown[0, XO["hu"]:XO["hu"] + 8 * 128 * 2].rearrange("(c p s) -> c p s", c=8, p=128)
            k.op("pool", lambda e: e.dma_start(out=ha[:, :, 0:15], in_=aT[:, :, 0:15]), reads=[r_aT], writes=[r_xown], dma=r_xown)
            k.op("pool", lambda e: e.dma_start(out=ha[:, :, 15:30], in_=aT[:, :, NL - 15:NL]), reads=[r_aT], writes=[r_xown], dma=r_xown)
            k.op("pool", lambda e: e.dma_start(out=hu[:, :, 0:1], in_=u2T[:, :, 0:1], allow_slow_non_contiguous=True), reads=[r_u2], writes=[r_xown], dma=r_xown)
            k.op("pool", lambda e: e.dma_start(out=hu[:, :, 1:2], in_=u2T[:, :, NL - 1:NL], allow_slow_non_contiguous=True), reads=[r_u2], writes=[r_xown], dma=r_xown)
            k.op("pool", lambda e: e.collective_compute("AllGather", ALU.bypass, replica_groups=[list(range(NCORES))],
                 ins=[xown.ap().opt()], outs=[xall.ap().opt()]), reads=[r_xown], writes=[r_xall], sk="cc")
        k.barrier()

    def attn_phase(l):
        last = (l == L - 1)
        NKT = NK // 128
        NLT = NL // 128
        TQ = NL if last else T
        with ExitStack() as st:
            kn = [sb(st, f"kn{i}", [128, NK], BF16) for i in range(2)]
            kp = [sb(st, f"kp{i}", [64, NK], BF16) for i in range(2)]
            vt = [sb(st, f"vt{i}", [128, NKT, 128], BF16) for i in range(2)]
            qn = [sb(st, f"qn{i}", [128, T], BF16) for i in range(2)]
            qp = [sb(st, f"qp{i}", [64, T], BF16) for i in range(2)]
            pt = [sb(st, f"pt{i}", [128, 512], BF16) for i in range(4)]
            rc, r_rc = sb(st, "rc", [128, 512], F32)
            oo = [sb(st, f"oo{i}", [128, 512], BF16) for i in range(2)]
            cnt = {"kv": 0, "q": 0, "p": 0, "o": 0, "s": 0}
            stgk = [sb(st, f"stgk{i}", [128, NL], BF16) for i in range(4)]

            def load_kv(key_n, rows_n, hd_n, key_v, vcols, hd_v, with_pe, hd_p=None):
                i = cnt["kv"] % 2
                cnt["kv"] += 1
                (knt, rkn), (kpt, rkp), (vtt, rvt) = kn[i], kp[i], vt[i]

                def kview(buf, off, rows, hd, nt):
                    b = off + hd * rows * nt
                    return buf[:, b:b + rows * nt].rearrange("r (p t) -> p r t", t=nt)

                def blend(dstv, rdst, srcA, srcB, rows, as3=False):
                    (sA, rA), (sB, rB) = stgk[cnt["s"] % 4], stgk[(cnt["s"] + 1) % 4]
                    cnt["s"] += 2
                    vA = sA[0:rows, :].rearrange("p (t d) -> p t d", d=128) if as3 else sA[0:rows, :]
                    vB = sB[0:rows, :].rearrange("p (t d) -> p t d", d=128) if as3 else sB[0:rows, :]
                    load(vA, srcA, rA, r_xall)
                    load(vB, srcB, rB, r_xall)
                    ts(dstv, vA, sel[0:rows, 0:1], None, ALU.mult, None, [rA, r_sel], [rdst])
                    stt(dstv, vB, sel[0:rows, 1:2], dstv, ALU.mult, ALU.add, [rB, r_sel, rdst], [rdst])

                kvn = kview(xall, XO[key_n], 128, hd_n, NL)
                vsrc = xall[:, XO[key_v]:XO[key_v] + NL * vcols].rearrange("r (t p d) -> p r t d", p=128, d=vcols)[:, :, :, hd_v * 128:(hd_v + 1) * 128]
                for j in range(4):
                    blend(knt[:, j * NL:(j + 1) * NL], rkn, kvn[:, j], kvn[:, 4 + j], 128)
                    if with_pe:
                        kvp = kview(xall, XO["kbp"], 64, hd_p, NL)
                        blend(kpt[:, j * NL:(j + 1) * NL], rkp, kvp[:, j], kvp[:, 4 + j], 64)
                    blend(vtt[:, j * NLT:(j + 1) * NLT, :], rvt, vsrc[:, j], vsrc[:, 4 + j], 128, as3=True)
                load(knt[:, 4 * NL:NK].rearrange("p (r t) -> p r t", r=1), kview(cown, CO[key_n], 128, hd_n, NCTX), rkn, r_cown)
                if with_pe:
                    load(kpt[:, 4 * NL:NK].rearrange("p (r t) -> p r t", r=1), kview(cown, CO["kbp"], 64, hd_p, NCTX), rkp, r_cown)
                vsrc = cown[:, CO[key_v]:CO[key_v] + NCTX * vcols].rearrange("r (t p d) -> p r t d", p=128, d=vcols)[:, :, :, hd_v * 128:(hd_v + 1) * 128]
                load(vtt[:, 4 * NLT:NKT, :], vsrc[:, 0], rvt, r_cown)
                return (knt, rkn), (kpt, rkp), (vtt, rvt)

            def attend(kvs, qnt, rqn, qpt, rqp, with_pe, out_head):
                (knt, rkn), (kpt, rkp), (vtt, rvt) = kvs
                for (t0, n, isctx) in chunks:
                    if isctx and last:
                        continue
                    ktiles = list(range(4 * NLT, NKT)) if isctx else list(range(NKT))
                    po, rpo = PB[4 + cnt["o"] % 2]
                    psm, rps = PB[6 + cnt["o"] % 2]
                    nk_ = len(ktiles)
                    pend = []

                    def s_mm(kt):
                        pS, rpS = PB[cnt["p"] % 4]
                        pairs = [(knt[:, kt * 128:(kt + 1) * 128], qnt[:, t0:t0 + n])]
                        rds = [rkn, rqn]
                        if with_pe:
                            pairs.append((kpt[:, kt * 128:(kt + 1) * 128], qpt[:, t0:t0 + n]))
                            rds += [rkp, rqp]
                        mm(pS[:, 0:n], pairs, rds, rpS)
                        ptt, rpt = pt[cnt["p"] % 4]
                        cnt["p"] += 1
                        act(ptt[:, 0:n], pS[:, 0:n], AF.Exp, [rpS], [rpt])
                        return ptt, rpt

                    def pv_mm(idx, kt, ptt, rpt, po=po, psm=psm, rpo=rpo, rps=rps, n=n, nk_=nk_):
                        k.op("pe", lambda e, po=po, n=n, nk_=nk_: e.matmul(po[:, 0:n], vtt[:, kt, :], ptt[:, 0:n], start=(idx == 0), stop=(idx == nk_ - 1)),
                             reads=[rvt, rpt], writes=[rpo], inc=False)
                        k.op("pe", lambda e, psm=psm, n=n, nk_=nk_: e.matmul(psm[:, 0:n], onesb[:, :], ptt[:, 0:n], start=(idx == 0), stop=(idx == nk_ - 1)),
                             reads=[r_onesb, rpt, rvt], writes=[rps, rpo], inc=True)
                    for idx, kt in enumerate(ktiles):
                        pend.append((idx, kt) + s_mm(kt))
                        if len(pend) > 2:
                            pv_mm(*pend.pop(0))
                    while pend:
                        pv_mm(*pend.pop(0))
                    k.op("dve", lambda e, n=n, psm=psm: e.reciprocal(rc[:, 0:n], psm[:, 0:n]), reads=[rps], writes=[r_rc])
                    ot, rot = oo[cnt["o"] % 2]
                    cnt["o"] += 1
                    tt(ot[:, 0:n], po[:, 0:n], rc[:, 0:n], ALU.mult, [rpo, r_rc], [rot])
                    store(att[out_head, :, t0:t0 + n], ot[:, 0:n], r_att, rot)

            for hd in range(8):
                kvs = load_kv("kbn", 128, hd, "vb", 1024, hd, True, hd)
                i = cnt["q"] % 2
                cnt["q"] += 1
                (qnt, rqn), (qpt, rqp) = qn[i], qp[i]
                load(qnt[:, 0:TQ], qb[hd, 0:128, 0:TQ], rqn, r_qb)
                load(qpt[:, 0:TQ], qb[hd, 128:192, 0:TQ], rqp, r_qb)
                attend(kvs, qnt, rqn, qpt, rqp, True, hd)
            for kvh in range(2):
                kvs = load_kv("kc", 128, kvh, "vc", 256, kvh, False)
                for g in range(4):
                    hd = kvh * 4 + g
                    i = cnt["q"] % 2
                    cnt["q"] += 1
                    (qnt, rqn) = qn[i]
                    load(qnt[:, 0:TQ], qc[hd, :, 0:TQ], rqn, r_qc)
                    attend(kvs, qnt, rqn, None, None, False, 8 + hd)
        k.barrier()

    def conv_phase(l):
        last = (l == L - 1)
        with ExitStack() as st:
            ain, r_ain = sb(st, "ain", [128, 8, 512 + 30], BF16)
            uin, r_uin = sb(st, "uin", [128, 8, 512 + 2], BF16)
            bgi, r_bgi = sb(st, "bgi", [128, 8, 512], BF16)
            cvo, r_cvo = sb(st, "cvo", [128, 8, 512], F32)
            lao, r_lao = sb(st, "lao", [128, 8, 512], BF16)
            ydo, r_ydo = sb(st, "ydo", [128, 8, 512], BF16)
            dg, r_dg = sb(st, "dg", [128, 8, 31, 128], BF16)
            dgd, r_dgd = sb(st, "dgd", [128, 8, 3, 128], BF16)
            hal, r_hal = sb(st, "hal", [128, NCORES, 8, 30], BF16)
            hul, r_hul = sb(st, "hul", [128, NCORES, 8, 2], BF16)
            hz, r_hz = sb(st, "hz", [128, 8, 15], F32)
            sq = [sb(st, f"sq{i}", [128, 512], F32) for i in range(2)]
            mean, r_mean = sb(st, "mean", [128, 512], F32)
            rstd, r_rstd = sb(st, "rstd", [128, 512], F32)
            tmpf = [sb(st, f"tmpf{i}", [128, 512], F32) for i in range(2)]
            for ch in range(8):
                for j in range(31):
                    ts(dg[:, ch, j, :], identb[:, :], vcol(l, V_WDWA + ch * 31 + j), None, ALU.mult, None, [r_identb] + r_const, [r_dg])
                for j in range(3):
                    ts(dgd[:, ch, j, :], identb[:, :], vcol(l, V_WDWD + ch * 3 + j), None, ALU.mult, None, [r_identb] + r_const, [r_dgd])
            for r in range(NCORES):
                hsrc = xall[r, XO["ha"]:XO["ha"] + 8 * 128 * 30].rearrange("(c p s) -> p c s", c=8, p=128)
                load(hal[:, r, :, :], hsrc, r_hal, r_xall)
                hsrc = xall[r, XO["hu"]:XO["hu"] + 8 * 128 * 2].rearrange("(c p s) -> p c s", c=8, p=128)
                load(hul[:, r, :, :], hsrc, r_hul, r_xall)
            aL, r_aL = sb(st, "aL", [128, 8, 15], F32)
            aR, r_aR = sb(st, "aR", [128, 8, 15], F32)
            uL, r_uL = sb(st, "uL", [128, 8, 1], F32)
            uR, r_uR = sb(st, "uR", [128, 8, 1], F32)
            for (dstt, rd, srcs, rsrc, lo, hi, s0) in ((aL, r_aL, hal, r_hal, 15, 30, 2), (aR, r_aR, hal, r_hal, 0, 15, 10),
                                                      (uL, r_uL, hul, r_hul, 1, 2, 2), (uR, r_uR, hul, r_hul, 0, 1, 10)):
                for r in range(NCORES):
                    if r == 0:
                        ts(dstt[:, :, :], srcs[:, r, :, lo:hi], sel[:, s0 + r:s0 + r + 1], None, ALU.mult, None, [rsrc, r_sel], [rd])
                    else:
                        stt(dstt[:, :, :], srcs[:, r, :, lo:hi], sel[:, s0 + r:s0 + r + 1], dstt[:, :, :], ALU.mult, ALU.add, [rsrc, r_sel, rd], [rd])
            bk = [0]
            for (t0, n, isctx) in chunks:
                if isctx and last:
                    continue
                seq0, seq1 = (NL, T) if isctx else (0, NL)
                lo, hi = max(seq0, t0 - 15), min(seq1, t0 + n + 15)
                k.op("dve", lambda e: e.memset(ain[:, :, :], 0.0), writes=[r_ain])
                k.op("dve", lambda e: e.memset(uin[:, :, :], 0.0), writes=[r_uin])
                load(ain[:, :, 15 - (t0 - lo):15 + (hi - t0)], aT.ap().rearrange("c p t -> p c t")[:, :, lo:hi], r_ain, r_aT)
                lo1, hi1 = max(seq0, t0 - 1), min(seq1, t0 + n + 1)
                load(uin[:, :, 1 - (t0 - lo1):1 + (hi1 - t0)], u2T.ap().rearrange("c p t -> p c t")[:, :, lo1:hi1], r_uin, r_u2)
                load(bgi[:, :, 0:n], bgT.ap().rearrange("c p t -> p c t")[:, :, t0:t0 + n], r_bgi, r_bg)
                if not isctx and t0 == 0:
                    cp(ain[:, :, 0:15], aL[:, :, :], [r_aL], [r_ain])
                    cp(uin[:, :, 0:1], uL[:, :, :], [r_uL], [r_uin])
                if not isctx and t0 + n == NL:
                    cp(ain[:, :, 15 + n:30 + n], aR[:, :, :], [r_aR], [r_ain])
                    cp(uin[:, :, 1 + n:2 + n], uR[:, :, :], [r_uR], [r_uin])
                for ch in range(8):
                    pbt, rpb = PB[bk[0] % 4]
                    bk[0] += 1
                    mm(pbt[:, 0:n], [(dg[:, ch, j, :], ain[:, ch, j:j + n]) for j in range(31)], [r_dg, r_ain], rpb)
                    act(cvo[:, ch, 0:n], pbt[:, 0:n], AF.Identity, [rpb] + r_const, [r_cvo], bias=vcol(l, V_BDWA + ch))
                pm, rpm = PB[4]
                for ch in range(8):
                    k.op("pe", lambda e, ch=ch, n=n, pm=pm: e.matmul(pm[:, 0:n], ones[:, :], cvo[:, ch, 0:n], start=(ch == 0), stop=(ch == 7)),
                         reads=[r_cvo, r_ones], writes=[rpm], inc=(ch == 7))
                ts(mean[:, 0:n], pm[:, 0:n], 1.0 / 1024, None, ALU.mult, None, [rpm], [r_mean])
                for ch in range(8):
                    tt(cvo[:, ch, 0:n], cvo[:, ch, 0:n], mean[:, 0:n], ALU.subtract, [r_cvo, r_mean], [r_cvo])
                rms_stats([cvo[:, ch, 0:n] for ch in range(8)], r_cvo, 8, n, 128, sq, rstd, r_rstd, PB[5], 1024, EPS, 1.0 / 1024)
                for ch in range(8):
                    tf, rtf = tmpf[ch % 2]
                    stt(tf[:, 0:n], cvo[:, ch, 0:n], vcol(l, V_GLNA + ch), rstd[:, 0:n], ALU.mult, ALU.mult, [r_cvo, r_rstd] + r_const, [rtf])
                    act(lao[:, ch, 0:n], tf[:, 0:n], AF.Silu, [rtf] + r_const, [r_lao], bias=vcol(l, V_BLNA + ch))
                store(laT.ap().rearrange("c p t -> p c t")[:, :, t0:t0 + n], lao[:, :, 0:n], r_la, r_lao)
                for ch in range(8):
                    pbt, rpb = PB[bk[0] % 4]
                    bk[0] += 1
                    mm(pbt[:, 0:n], [(dgd[:, ch, j, :], uin[:, ch, j:j + n]) for j in range(3)], [r_dgd, r_uin], rpb)
                    tt(ydo[:, ch, 0:n], pbt[:, 0:n], bgi[:, ch, 0:n], ALU.mult, [rpb, r_bgi], [r_ydo])
                store(ydT.ap().rearrange("c p t -> p c t")[:, :, t0:t0 + n], ydo[:, :, 0:n], r_yd, r_ydo)
        k.barrier()

    def merge_phase(l):
        last = (l == L - 1)
        with ExitStack() as st:
            xm, r_xmS = sb(st, "xm", [128, DC, 512], BF16)
            bin_ = [sb(st, f"bin{i}", [128, 8, 512], BF16) for i in range(4)]
            ys, r_ys = sb(st, "ys", [128, DC, 512], F32)
            yn, r_yn = sb(st, "yn", [128, DC, 512], BF16)
            sg = [sb(st, f"sg{i}", [128, 512], F32) for i in range(2)]
            t2 = [sb(st, f"t2{i}", [128, 512], F32) for i in range(2)]
            wg_ = [sb(st, f"wgt{i}", [128, DC, 512], BF16) for i in range(2)]
            wb2 = [sb(st, f"wb2{i}", [128, 8, 512], BF16) for i in range(2)]
            wi_ = [0, 0]
            bsrc = [(laT, r_la, "woa", None), (att, r_att, "wob", 0), (att, r_att, "woc", 8), (ydT, r_yd, "wod", None)]
            for (t0, n, isctx) in chunks:
                if isctx and last:
                    continue
                load(xm[:, :, 0:n], xmT.ap().rearrange("c p t -> p c t")[:, :, t0:t0 + n], r_xmS, r_xm)
                for bi, (srct, rs, wn, h0) in enumerate(bsrc):
                    v = srct.ap().rearrange("c p t -> p c t")
                    v = v[:, :, t0:t0 + n] if h0 is None else v[:, h0:h0 + 8, t0:t0 + n]
                    load(bin_[bi][0][:, :, 0:n], v, bin_[bi][1], rs)
                for bi, (srct, rs, wn, h0) in enumerate(bsrc):
                    bt, rbt = bin_[bi]
                    for mb in range(4):
                        wgt, rwg = wg_[wi_[0] % 2]
                        wi_[0] += 1
                        load_w(wgt, rwg, "win", l, 7744 + bi * 2048 + mb * 512, 512)
                        wbt, rwb = wb2[wi_[1] % 2]
                        wi_[1] += 1
                        load_w(wbt, rwb, wn, l, mb * 512, 512)
                        for mm_ in range(4):
                            m = mb * 4 + mm_
                            pg, rpg = PB[m % 2]
                            py, rpy = PB[2 + m % 2]
                            mm(pg[:, 0:n], [(wgt[:, kc, mm_ * 128:(mm_ + 1) * 128], xm[:, kc, 0:n]) for kc in range(DC)], [rwg, r_xmS], rpg)
                            mm(py[:, 0:n], [(wbt[:, kc, mm_ * 128:(mm_ + 1) * 128], bt[:, kc, 0:n]) for kc in range(8)], [rwb, rbt], rpy)
                            sgt, rsg = sg[m % 2]
                            act(sgt[:, 0:n], pg[:, 0:n], AF.Sigmoid, [rpg], [rsg])
                            if bi == 0:
                                tt(ys[:, m, 0:n], sgt[:, 0:n], py[:, 0:n], ALU.mult, [rsg, rpy], [r_ys])
                            else:
                                t2t, rt2 = t2[m % 2]
                                tt(t2t[:, 0:n], sgt[:, 0:n], py[:, 0:n], ALU.mult, [rsg, rpy], [rt2])
                                tt(ys[:, m, 0:n], ys[:, m, 0:n], t2t[:, 0:n], ALU.add, [r_ys, rt2], [r_ys])
                for m in range(DC):
                    cp(yn[:, m, 0:n], ys[:, m, 0:n], [r_ys], [r_yn], eng=("dve", "act")[m % 2])
                load(ys[:, :, 0:n], hT.ap().rearrange("c p t -> p c t")[:, :, t0:t0 + n], r_ys, r_hT)
                for mb in range(4):
                    wgt, rwg = wg_[wi_[0] % 2]
                    wi_[0] += 1
                    load_w(wgt, rwg, "wo", l, mb * 512, 512)
                    for mm_ in range(4):
                        m = mb * 4 + mm_
                        py, rpy = PB[4 + m % 2]
                        mm(py[:, 0:n], [(wgt[:, kc, mm_ * 128:(mm_ + 1) * 128], yn[:, kc, 0:n]) for kc in range(DC)], [rwg, r_yn], rpy)
                        stt(ys[:, m, 0:n], py[:, 0:n], mcol(isctx, l, 5, m), ys[:, m, 0:n], ALU.mult, ALU.add, [rpy, r_ys] + r_const, [r_ys])
                store(hT.ap().rearrange("c p t -> p c t")[:, :, t0:t0 + n], ys[:, :, 0:n], r_hT, r_ys)
        k.barrier()

    for l in range(L):
        for pi, ph in enumerate((lambda: ffn_phase(l, 1), lambda: proj_phase(l), lambda: attn_phase(l),
                                 lambda: conv_phase(l), lambda: merge_phase(l), lambda: ffn_phase(l, 2))):
            if stop >= 4 + pi:
                ph()

    with ExitStack() as st:
        hi_ = [sb(st, f"hi{i}", [128, DC, 128], F32) for i in range(2)]
        yo = [sb(st, f"yo{i}", [128, D], F32) for i in range(2)]
        for ti in range(NL // 128):
            (ht, rht), (yt, ryt) = hi_[ti % 2], yo[ti % 2]
            load(ht[:, :, :], hT.ap().rearrange("c p t -> p c t")[:, :, ti * 128:(ti + 1) * 128], rht, r_hT)
            for g in range(4):
                pbt, rpb = PB[(ti * 4 + g) % 8]
                for j in range(4):
                    dc = g * 4 + j
                    tr(pbt[:, j * 128:(j + 1) * 128], ht[:, dc, :], ident[:, :], [rht, r_ident], rpb)
                cp(yt[:, g * 512:(g + 1) * 512], pbt[:, :], [rpb], [ryt], eng=("dve", "act")[g % 2])
            store(yout[ti * 128:(ti + 1) * 128, :], yt[:, :], r_yout, ryt)
    k.barrier()
    k.emit(es)
    es.close()
    return nc


def _rope_tables(pos, d_rot, n_ctx):
    row = (pos // GRID_W).astype(np.float32)
    col = (pos % GRID_W).astype(np.float32)
    n_freq = d_rot // 4
    inv = (10000.0 ** (-np.arange(n_freq, dtype=np.float32) / n_freq)).astype(np.float32)
    ang = np.concatenate([row[:, None] * inv, col[:, None] * inv], axis=-1)
    cos, sin = np.cos(ang).T, np.sin(ang).T
    cos2 = np.concatenate([cos, cos], 0)
    sin2 = np.concatenate([-sin, sin], 0)
    cos2 = np.concatenate([cos2, np.ones((d_rot, n_ctx), np.float32)], 1)
    sin2 = np.concatenate([sin2, np.zeros((d_rot, n_ctx), np.float32)], 1)
    return np.ascontiguousarray(cos2, np.float32), np.ascontiguousarray(sin2, np.float32)


def _fm(v):
    return np.ascontiguousarray(v.reshape(-1, 128).T)


_NC_CACHE = {}


def kernel(x, c, ctx, c_ctx, w_mod, b_mod, w_ffn1_in, w_ffn1_out, w_ffn2_in, w_ffn2_out,
           w_in, g_q_lora, w_uq, g_kv_lora, w_ukv, g_q_b, g_k_b, w_o_b, g_q_c, g_k_c, w_o_c,
           w_dw_a, b_dw_a, g_ln_a, b_ln_a, w_out_a, w_dw_d, w_out_d, w_o):
    x = np.asarray(x)
    B, S, _ = x.shape
    L = np.asarray(w_mod).shape[0]
    NL = S // 4
    key = (NL, L)
    if key not in _NC_CACHE:
        _NC_CACHE[key] = build(NL, L, int(os.environ.get('KSTOP', 99)))
    nc = _NC_CACHE[key]
    f = lambda a: np.asarray(a, dtype=np.float32)
    c, ctx, c_ctx = f(c), f(ctx), f(c_ctx)
    ident = np.eye(128, dtype=np.float32)
    cv = np.stack([c[0], c[1], c_ctx], 0)
    cvecT = np.ascontiguousarray(cv.reshape(3, DC, 128).transpose(2, 1, 0).reshape(128, DC * 3))
    vec = np.zeros((L, 128, NV), np.float32)
    pad = lambda v: np.concatenate([v, np.zeros(128 - v.shape[0], np.float32)])
    sw = lambda v, hlf: np.concatenate([v[hlf:], v[:hlf]])
    for l in range(L):
        vec[l, :, V_GQL:V_GQL + 4] = _fm(f(g_q_lora)[l])
        vec[l, :, V_GKVL:V_GKVL + 4] = _fm(f(g_kv_lora)[l])
        gq, gk = f(g_q_b)[l], f(g_k_b)[l]
        vec[l, :, V_GQBN] = gq[:128]
        vec[l, :, V_GQBP] = pad(gq[128:])
        vec[l, :, V_GQBR] = pad(sw(gq[128:], 32))
        vec[l, :, V_GKBN] = gk[:128]
        vec[l, :, V_GKBP] = pad(gk[128:])
        vec[l, :, V_GKBR] = pad(sw(gk[128:], 32))
        vec[l, :, V_GQC] = f(g_q_c)[l]
        vec[l, :, V_GQCR] = sw(f(g_q_c)[l], 64)
        vec[l, :, V_GKC] = f(g_k_c)[l]
        vec[l, :, V_GKCR] = sw(f(g_k_c)[l], 64)
        vec[l, :, V_BDWA:V_BDWA + 8] = _fm(f(b_dw_a)[l])
        vec[l, :, V_GLNA:V_GLNA + 8] = _fm(f(g_ln_a)[l])
        vec[l, :, V_BLNA:V_BLNA + 8] = _fm(f(b_ln_a)[l])
        wd = f(w_dw_d)[l]
        vec[l, :, V_WDWD:V_WDWD + 24] = wd.reshape(3, 8, 128).transpose(2, 1, 0).reshape(128, 24)
        wa = f(w_dw_a)[l]
        vec[l, :, V_WDWA:V_WDWA + 248] = wa.reshape(31, 8, 128).transpose(2, 1, 0).reshape(128, 248)
    vecs = np.ascontiguousarray(vec.transpose(1, 0, 2).reshape(128, L * NV))
    rowsh = lambda w, R, r: np.ascontiguousarray(f(w)[:, r * R:(r + 1) * R, :].reshape(L * R, -1))
    colsh = lambda w, r: np.ascontiguousarray(f(w)[:, :, r * 256:(r + 1) * 256].reshape(L * DFF, 256))
    in_maps = []
    for r in range(NCORES):
        b, j = r // 4, r % 4
        pos = np.arange(j * NL, (j + 1) * NL)
        cosB, sinB = _rope_tables(pos, 64, NCTX)
        cosC, sinC = _rope_tables(pos, 128, NCTX)
        sel = np.zeros((128, 18), np.float32)
        sel[:, b] = 1.0
        if j > 0:
            sel[:, 2 + r - 1] = 1.0
        if j < 3:
            sel[:, 10 + r + 1] = 1.0
        m = {
            "xin": np.ascontiguousarray(x[b, j * NL:(j + 1) * NL, :], dtype=np.float32),
            "cin": np.ascontiguousarray(ctx[b]),
            "cvecT": cvecT,
            "wmod": np.ascontiguousarray(f(w_mod)[:, :, r * 2304:(r + 1) * 2304].reshape(L * D, 2304)),
            "bmodT": np.ascontiguousarray(f(b_mod)[:, r * 2304:(r + 1) * 2304].reshape(L, 18, 128).transpose(2, 0, 1).reshape(128, L * 18)),
            "vecs": vecs, "sel": sel, "ident": ident,
            "cosB": cosB, "sinB": sinB, "cosC": cosC, "sinC": sinC,
            "w_f1i": rowsh(w_ffn1_in, 256, r), "w_f1o": colsh(w_ffn1_out, r),
            "w_f2i": rowsh(w_ffn2_in, 256, r), "w_f2o": colsh(w_ffn2_out, r),
            "w_win": rowsh(w_in, 256, r), "w_wukv": rowsh(w_ukv, 64, r), "w_wuq": rowsh(w_uq, 64, r),
            "w_woa": rowsh(w_out_a, 128, r), "w_wob": rowsh(w_o_b, 128, r), "w_woc": rowsh(w_o_c, 128, r),
            "w_wod": rowsh(w_out_d, 128, r), "w_wo": rowsh(w_o, 256, r),
        }
        in_maps.append(m)
    res = run_bass_kernel_spmd(nc, in_maps, core_ids=list(range(NCORES)))
    out = np.zeros((B, S, D), np.float32)
    for r in range(NCORES):
        b, j = r // 4, r % 4
        out[b, j * NL:(j + 1) * NL, :] = res.results[r]["yout"]
    return out
```
